# Optimizing a Trainium2 kernel written in Bass

```python
import jax, jax.numpy as jnp
from jax import lax
import numpy as np

D_MODEL = 1024
BATCH = 32
SEQ = 256
DEPTH = 4
DEC_BATCH = 2
DEC_SEQ = 2048
PAST_LEN = 512

GRID_W = 64
N_MIXERS = 4
N_ATTN = (DEPTH + 3) // 4
N_SGU = (DEPTH + 2) // 4
N_SCONV = (DEPTH + 1) // 4
N_FOURIER = DEPTH // 4
N_HEADS = 16
N_KV_HEADS = 4
HEAD_DIM = 64
Q_PER_KV = N_HEADS // N_KV_HEADS
WINDOW = 128
BLOCK = 128
ROPE_THETA = 10000.0
SGU_CHUNK = 128
SGU_GROUPS = 8
SGU_GROUP_DIM = D_MODEL // SGU_GROUPS
FOURIER_GROUPS = 8
FOURIER_GROUP_DIM = D_MODEL // FOURIER_GROUPS
CONV_W = 3
D_FF = 2816
EPS = 1e-6
NEG_BIG = -1e30

kernel_name = 'hybrid_diffusion_prefix_step'


def rms_norm(x, g):
    xf = x.astype(jnp.float32)
    y = xf * lax.rsqrt(jnp.mean(xf * xf, axis=-1, keepdims=True) + EPS)
    return (y * g.astype(jnp.float32)).astype(x.dtype)


def layer_norm(x, g):
    xf = x.astype(jnp.float32)
    mu = jnp.mean(xf, axis=-1, keepdims=True)
    xc = xf - mu
    y = xc * lax.rsqrt(jnp.mean(xc * xc, axis=-1, keepdims=True) + EPS)
    return (y * g.astype(jnp.float32)).astype(x.dtype)


def modulate(x, g, shift, scale):
    return rms_norm(x, g) * (1 + scale) + shift


def conv3_centred(x, w):
    xp = jnp.pad(x, ((0, 0), (1, 1), (0, 0)))
    return xp[:, :-2] * w[0] + xp[:, 1:-1] * w[1] + xp[:, 2:] * w[2]


def _rotate(x, pos):
    half = x.shape[-1] // 2
    inv = ROPE_THETA ** (-jnp.arange(half, dtype=jnp.float32) / half)
    ang = pos.astype(jnp.float32)[:, None] * inv[None, :]
    cos = jnp.cos(ang)[None, :, None, :]
    sin = jnp.sin(ang)[None, :, None, :]
    x1 = x[..., :half].astype(jnp.float32)
    x2 = x[..., half:].astype(jnp.float32)
    return jnp.concatenate([x1 * cos - x2 * sin, x2 * cos + x1 * sin], axis=-1).astype(x.dtype)


def axial_rope(x):
    t = jnp.arange(x.shape[1])
    row, col = t // GRID_W, t % GRID_W
    h = x.shape[-1] // 2
    return jnp.concatenate([_rotate(x[..., :h], row), _rotate(x[..., h:], col)], axis=-1)


def split_qkv(h, wqkv):
    B, T, _ = h.shape
    qkv = h @ wqkv
    nq, nk = N_HEADS * HEAD_DIM, N_KV_HEADS * HEAD_DIM
    q = qkv[..., :nq].reshape(B, T, N_HEADS, HEAD_DIM)
    k = qkv[..., nq:nq + nk].reshape(B, T, N_KV_HEADS, HEAD_DIM)
    v = qkv[..., nq + nk:].reshape(B, T, N_KV_HEADS, HEAD_DIM)
    return q, k, v


def context_attention(q, k, v, sink):
    B, S = q.shape[:2]
    nq = S // BLOCK
    qb = q.reshape(B, nq, BLOCK, N_KV_HEADS, Q_PER_KV, HEAD_DIM).transpose(1, 0, 2, 3, 4, 5)
    sink_l = sink.astype(jnp.float32).reshape(N_KV_HEADS, Q_PER_KV)[None, :, :, None, None]
    scale = HEAD_DIM ** -0.5

    def one_block(qblk):
        s = jnp.einsum('bqkgd,bskd->bkgqs', qblk, k).astype(jnp.float32) * scale
        s = jnp.concatenate([s, jnp.broadcast_to(sink_l, s.shape[:-1] + (1,))], axis=-1)
        p = jax.nn.softmax(s, axis=-1)[..., :-1].astype(v.dtype)
        return jnp.einsum('bkgqs,bskd->bqkgd', p, v)

    o = lax.map(one_block, qb)
    return o.transpose(1, 0, 2, 3, 4, 5).reshape(B, S, N_HEADS * HEAD_DIM)


def latent_attention(q, k, v, ck, cv, sink):
    B, T = q.shape[:2]
    nb = T // BLOCK
    P = ck.shape[1]
    L = 3 * BLOCK
    qb = q.reshape(B, nb, BLOCK, N_KV_HEADS, Q_PER_KV, HEAD_DIM)

    def band(x):
        xp = jnp.pad(x, ((0, 0), (BLOCK, BLOCK), (0, 0), (0, 0)))
        xp = xp.reshape(B, nb + 2, BLOCK, N_KV_HEADS, HEAD_DIM)
        return jnp.concatenate([xp[:, :-2], xp[:, 1:-1], xp[:, 2:]], axis=2)

    kb, vb = band(k), band(v)
    qpos = jnp.arange(nb)[:, None] * BLOCK + jnp.arange(BLOCK)[None, :]
    kpos = jnp.arange(nb)[:, None] * BLOCK - BLOCK + jnp.arange(L)[None, :]
    rel = kpos[:, None, :] - qpos[:, :, None]
    valid = (jnp.abs(rel) <= WINDOW) & (kpos[:, None, :] >= 0) & (kpos[:, None, :] < T)
    scale = HEAD_DIM ** -0.5
    s_loc = jnp.einsum('bnqkgd,bnskd->bnkgqs', qb, kb).astype(jnp.float32) * scale
    s_loc = jnp.where(valid[None, :, None, None], s_loc, NEG_BIG)
    s_ctx = jnp.einsum('bnqkgd,bskd->bnkgqs', qb, ck).astype(jnp.float32) * scale
    sink_l = sink.astype(jnp.float32).reshape(N_KV_HEADS, Q_PER_KV)[None, None, :, :, None, None]
    s = jnp.concatenate([s_loc, s_ctx, jnp.broadcast_to(sink_l, s_loc.shape[:-1] + (1,))], axis=-1)
    p = jax.nn.softmax(s, axis=-1)
    p_loc = p[..., :L].astype(v.dtype)
    p_ctx = p[..., L:L + P].astype(v.dtype)
    o = (jnp.einsum('bnkgqs,bnskd->bnqkgd', p_loc, vb)
         + jnp.einsum('bnkgqs,bskd->bnqkgd', p_ctx, cv))
    return o.reshape(B, T, N_HEADS * HEAD_DIM)


def attn_mixer_context(h, wqkv, wo, sink):
    q, k, v = split_qkv(h, wqkv)
    return context_attention(q, k, v, sink) @ wo, k, v


def attn_mixer_latent(h, ck, cv, wqkv, wo, sink):
    q, k, v = split_qkv(h, wqkv)
    q, k = axial_rope(q), axial_rope(k)
    return latent_attention(q, k, v, ck, cv, sink) @ wo


def sgu_mixer(h, w_in, ln_g, w_s, b_s, w_out):
    B, T, _ = h.shape
    z = jax.nn.gelu(h @ w_in)
    u, v = z[..., :D_MODEL], z[..., D_MODEL:]
    v = layer_norm(v, ln_g)
    n = T // SGU_CHUNK
    vc = v.reshape(B, n, SGU_CHUNK, SGU_GROUPS, SGU_GROUP_DIM)
    mixed = jnp.einsum('gpr,bnrgc->bnpgc', w_s, vc) + b_s.T[None, None, :, :, None]
    return (u * mixed.reshape(B, T, D_MODEL)) @ w_out


def short_conv_mixer(h, w_in, conv_w, w_out):
    p = h @ w_in
    b, cg, xin = p[..., :D_MODEL], p[..., D_MODEL:2 * D_MODEL], p[..., 2 * D_MODEL:]
    y = conv3_centred(cg * xin, conv_w)
    return (b * y) @ w_out


def fourier_mixer(h, w_out):
    B, T, _ = h.shape
    hg = h.astype(jnp.float32).reshape(B, T, FOURIER_GROUPS, FOURIER_GROUP_DIM)
    f = jnp.fft.fft2(hg, axes=(1, 3), norm='ortho').real
    return f.reshape(B, T, D_MODEL).astype(h.dtype) @ w_out


def conv_ffn(h, w_up, conv_w, w_down):
    a = conv3_centred(h @ w_up, conv_w)
    g, u = a[..., :D_FF], a[..., D_FF:]
    return (jax.nn.silu(g) * u) @ w_down


def setup_inputs(seed: int = 0) -> dict:
    key = jax.random.key(seed)
    ks = jax.random.split(key, 32)
    D = D_MODEL
    qkv_out = (N_HEADS + 2 * N_KV_HEADS) * HEAD_DIM

    def nrm(k, shape, scale):
        return jax.random.normal(k, shape, jnp.float32) * scale

    return {
        'x_prompt': nrm(ks[0], (BATCH, SEQ, D), 1.0),
        'x_sample': nrm(ks[1], (DEC_BATCH, DEC_SEQ, D), 1.0),
        'cache_k': nrm(ks[2], (DEC_BATCH, N_ATTN, PAST_LEN, N_KV_HEADS, HEAD_DIM), 1.0),
        'cache_v': nrm(ks[3], (DEC_BATCH, N_ATTN, PAST_LEN, N_KV_HEADS, HEAD_DIM), 1.0),
        'c': nrm(ks[4], (DEC_BATCH, D), 1.0),
        'c_ctx': nrm(ks[5], (D,), 1.0),
        'ada_w': nrm(ks[6], (DEPTH, D, 6 * D), 0.5 * D ** -0.5),
        'ada_b': nrm(ks[7], (DEPTH, 6 * D), 0.02),
        'norm_mix_g': 1.0 + nrm(ks[8], (DEPTH, D), 0.02),
        'norm_ffn_g': 1.0 + nrm(ks[9], (DEPTH, D), 0.02),
        'final_g': 1.0 + nrm(ks[10], (D,), 0.02),
        'attn_wqkv': nrm(ks[11], (N_ATTN, D, qkv_out), D ** -0.5),
        'attn_wo': nrm(ks[12], (N_ATTN, N_HEADS * HEAD_DIM, D), (N_HEADS * HEAD_DIM) ** -0.5),
        'attn_sink': nrm(ks[13], (N_ATTN, N_HEADS), 0.5),
        'sgu_w_in': nrm(ks[14], (N_SGU, D, 2 * D), D ** -0.5),
        'sgu_ln_g': 1.0 + nrm(ks[15], (N_SGU, D), 0.02),
        'sgu_w_s': nrm(ks[16], (N_SGU, SGU_GROUPS, SGU_CHUNK, SGU_CHUNK), SGU_CHUNK ** -0.5),
        'sgu_b_s': 1.0 + nrm(ks[17], (N_SGU, SGU_GROUPS, SGU_CHUNK), 0.02),
        'sgu_w_out': nrm(ks[18], (N_SGU, D, D), D ** -0.5),
        'sc_w_in': nrm(ks[19], (N_SCONV, D, 3 * D), D ** -0.5),
        'sc_conv': nrm(ks[20], (N_SCONV, CONV_W, D), CONV_W ** -0.5),
        'sc_w_out': nrm(ks[21], (N_SCONV, D, D), D ** -0.5),
        'fn_w_out': nrm(ks[22], (N_FOURIER, D, D), D ** -0.5),
        'ffn_w_up': nrm(ks[23], (DEPTH, D, 2 * D_FF), D ** -0.5),
        'ffn_conv': nrm(ks[24], (DEPTH, CONV_W, 2 * D_FF), CONV_W ** -0.5),
        'ffn_w_down': nrm(ks[25], (DEPTH, D_FF, D), D_FF ** -0.5),
    }


def reference(x_prompt, x_sample, cache_k, cache_v, c, c_ctx, ada_w, ada_b, norm_mix_g,
              norm_ffn_g, final_g, attn_wqkv, attn_wo, attn_sink, sgu_w_in, sgu_ln_g, sgu_w_s,
              sgu_b_s, sgu_w_out, sc_w_in, sc_conv, sc_w_out, fn_w_out, ffn_w_up, ffn_conv,
              ffn_w_down):
    xp, xs = x_prompt, x_sample
    new_k, new_v = [], []
    for l in range(DEPTH):
        kind, j = l % N_MIXERS, l // N_MIXERS
        mod_p = (jax.nn.silu(c_ctx) @ ada_w[l] + ada_b[l])[None, None, :]
        mod_s = (jax.nn.silu(c) @ ada_w[l] + ada_b[l])[:, None, :]
        sh1p, sc1p, g1p, sh2p, sc2p, g2p = jnp.split(mod_p, 6, axis=-1)
        sh1s, sc1s, g1s, sh2s, sc2s, g2s = jnp.split(mod_s, 6, axis=-1)
        hp = modulate(xp, norm_mix_g[l], sh1p, sc1p)
        hs = modulate(xs, norm_mix_g[l], sh1s, sc1s)
        if kind == 0:
            dp, kp, vp = attn_mixer_context(hp, attn_wqkv[j], attn_wo[j], attn_sink[j])
            new_k.append(kp)
            new_v.append(vp)
            ds = attn_mixer_latent(hs, cache_k[:, j], cache_v[:, j], attn_wqkv[j], attn_wo[j],
                                   attn_sink[j])
        elif kind == 1:
            dp = sgu_mixer(hp, sgu_w_in[j], sgu_ln_g[j], sgu_w_s[j], sgu_b_s[j], sgu_w_out[j])
            ds = sgu_mixer(hs, sgu_w_in[j], sgu_ln_g[j], sgu_w_s[j], sgu_b_s[j], sgu_w_out[j])
        elif kind == 2:
            dp = short_conv_mixer(hp, sc_w_in[j], sc_conv[j], sc_w_out[j])
            ds = short_conv_mixer(hs, sc_w_in[j], sc_conv[j], sc_w_out[j])
        else:
            dp = fourier_mixer(hp, fn_w_out[j])
            ds = fourier_mixer(hs, fn_w_out[j])
        xp = xp + g1p * dp
        xs = xs + g1s * ds
        hp = modulate(xp, norm_ffn_g[l], sh2p, sc2p)
        hs = modulate(xs, norm_ffn_g[l], sh2s, sc2s)
        xp = xp + g2p * conv_ffn(hp, ffn_w_up[l], ffn_conv[l], ffn_w_down[l])
        xs = xs + g2s * conv_ffn(hs, ffn_w_up[l], ffn_conv[l], ffn_w_down[l])
    y_prompt = rms_norm(xp, final_g)
    y_sample = rms_norm(xs, final_g)
    new_cache_k = jnp.stack(new_k, axis=1)
    new_cache_v = jnp.stack(new_v, axis=1)
    return (y_prompt, y_sample, new_cache_k, new_cache_v)
```

```python
import numpy as np
import concourse.bass as bass
import concourse.mybir as mybir
from concourse.bass_utils import run_bass_kernel_spmd

F32 = mybir.dt.float32
BF16 = mybir.dt.bfloat16
AF = mybir.ActivationFunctionType
ALU = mybir.AluOpType
AX = mybir.AxisListType

COMPUTE = ("pe", "act", "dve", "pool")

D = 1024
NCH = 8
DEPTH = 4
SEQ = 256
NPB = 4
NTOK = 1536
NBLK = 6
HW = 4 * 258 + 514
D_FF = 2816
NJ = 22
EPS = 1e-6
N_HEADS = 16
N_KV = 4
HD = 64
PAST = 512
WSLOT = 4096
NSLOT = 4


def hcol(b):
    return b * 258 if b < 4 else 1032 + (b - 4) * 256


class Prog:
    def __init__(self, nc, same_engine_sync=True):
        self.nc = nc
        self.eng = {"pe": nc.tensor, "act": nc.scalar, "dve": nc.vector,
                    "pool": nc.gpsimd, "sp": nc.sync}
        self.q = {e: [] for e in self.eng}
        self.sems = {}
        self.cnt = {}
        self.waited = {e: {} for e in self.eng}
        self.know = {}
        self.res = {}
        self.same_engine_sync = same_engine_sync
        self.nwaits = 0
        self.nops = 0
        for e in COMPUTE:
            self._sem(e)

    def _sem(self, key):
        if key not in self.sems:
            self.sems[key] = self.nc.alloc_semaphore(name="s_" + str(key))
            self.cnt[key] = 0
        return self.sems[key]

    def _deps(self, reads, writes):
        deps = []
        for r in reads:
            ent = self.res.get(r)
            if ent is not None and ent[0] is not None:
                deps.append(ent[0])
        for w in writes:
            ent = self.res.get(w)
            if ent is not None:
                if ent[0] is not None:
                    deps.append(ent[0])
                deps.extend(ent[1])
        return deps

    def _commit(self, tok, reads, writes):
        for r in reads:
            ent = self.res.get(r)
            if ent is None:
                self.res[r] = [None, [tok]]
            else:
                ent[1].append(tok)
        for w in writes:
            self.res[w] = [tok, []]

    def _plan_waits(self, e, deps):
        wd = self.waited[e]
        need = {}
        for (k, v) in deps:
            if k == e and (e == "pe" or not self.same_engine_sync):
                continue
            if wd.get(k, 0) >= v:
                continue
            if need.get(k, 0) < v:
                need[k] = v
        out = []
        for k, v in need.items():
            if wd.get(k, 0) >= v:
                continue
            out.append((k, v))
            wd[k] = v
            kn = self.know.get((k, v))
            if kn:
                for kk, vv in kn.items():
                    if wd.get(kk, 0) < vv:
                        wd[kk] = vv
        return out

    def op(self, e, fn, reads=(), writes=()):
        deps = self._deps(reads, writes)
        waits = self._plan_waits(e, deps)
        self.cnt[e] += 1
        tok = (e, self.cnt[e])
        self.know[tok] = dict(self.waited[e])
        self._commit(tok, reads, writes)
        sem = self.sems[e]
        sems = self.sems
        self.nwaits += len(waits)
        self.nops += 1

        def thunk(engine):
            for (k, v) in waits:
                engine.wait_ge(sems[k], v)
            last = fn(engine)
            last.then_inc(sem, 1)
        self.q[e].append(thunk)
        return tok

    def dma(self, e, dkey, out=None, in_=None, reads=(), writes=(), n=1, fn=None, inc=16):
        self._sem(dkey)
        deps = self._deps(reads, writes)
        waits = self._plan_waits(e, deps)
        self.cnt[dkey] += inc * n
        tok = (dkey, self.cnt[dkey])
        self.know[tok] = dict(self.waited[e])
        self._commit(tok, reads, writes)
        dsem = self.sems[dkey]
        sems = self.sems
        self.nwaits += len(waits)

        def thunk(engine):
            for (k, v) in waits:
                engine.wait_ge(sems[k], v)
            if fn is not None:
                fn(engine, dsem)
            else:
                engine.dma_start(out=out, in_=in_).then_inc(dsem, 16)
        self.q[e].append(thunk)
        return tok

    def final_wait(self, e, toks):
        waits = self._plan_waits(e, toks)
        sems = self.sems

        def thunk(engine):
            for (k, v) in waits:
                engine.wait_ge(sems[k], v)
        self.q[e].append(thunk)

    def emit(self):
        nc = self.nc
        q = self.q
        with nc.Block() as block:
            @block.tensor
            def _(eng):
                for t in q["pe"]:
                    t(eng)

            @block.scalar
            def _(eng):
                for t in q["act"]:
                    t(eng)

            @block.vector
            def _(eng):
                for t in q["dve"]:
                    t(eng)

            @block.gpsimd
            def _(eng):
                for t in q["pool"]:
                    t(eng)

            @block.sync
            def _(eng):
                for t in q["sp"]:
                    t(eng)


class Builder:
    def __init__(self, cfg):
        self.cfg = cfg
        nc = bass.Bass("TRN2", target_bir_lowering=False)
        self.nc = nc
        self.p = Prog(nc)
        self.din = {}
        self.uid = 0
        self.out_toks = []
        self.ncc = 0

    def inp(self, name, shape, dt=F32):
        t = nc_t = self.nc.dram_tensor(name, list(shape), dt, kind="ExternalInput")
        self.din[name] = t.ap()
        return t.ap()

    def outp(self, name, shape, dt=F32):
        return self.nc.dram_tensor(name, list(shape), dt, kind="ExternalOutput").ap()

    def sb(self, name, shape, dt=F32):
        return self.nc.alloc_sbuf_tensor("sb_" + name, list(shape), dt)

    def u(self, s):
        self.uid += 1
        return f"{s}{self.uid}"

    def load_const(self, name, dram_ap, shape, dt=F32, eng="sp", cast=False):
        t = self.sb(name, shape, dt)
        q = "pool" if cast else eng
        idx = (slice(None),) * len(shape)
        self.p.dma(q, "dc_" + name, out=t[idx], in_=dram_ap, writes=[name])
        return t

    def bankA(self):
        b = self.rotA[self.iA % len(self.rotA)]
        self.iA += 1
        return b

    def bankB(self):
        b = self.rotB[self.iB % len(self.rotB)]
        self.iB += 1
        return b

    def wload(self, parts):
        s = self.wi % NSLOT
        self.wi += 1
        off = 0
        views = []
        n = len(parts)
        first = True
        for (ap, k, ncols) in parts:
            v = self.wring[:, s, off:off + k * ncols].rearrange("p (k n) -> p k n", k=k)
            views.append(v)
            src = ap.rearrange("(k p) n -> p k n", p=128)
            self.p.dma("pool", f"dw{s}", out=v, in_=src, writes=[("w", s)] if first else [],
                       reads=[] if first else [])
            first = False
            off += k * ncols
        assert off <= WSLOT
        tok = (f"dw{s}", self.p.cnt[f"dw{s}"])
        self.p.res[("w", s)] = [tok, []]
        return s, views

    def build(self):
        nc, p, cfg = self.nc, self.p, self.cfg
        NL = cfg.get("nl", DEPTH)
        specs = {
            "xp_d": ("xp", [1024, D]),
            "xs_d": ("xs", [512, D]),
            "xsh_d": ("xsh", [256, D]),
            "ck_d": ("ck", [PAST, 256]),
            "cv_d": ("cv", [PAST, 256]),
            "cT_d": ("cT", [128, 16]),
            "ada_w_d": ("ada_w", [DEPTH, D, 6 * D]),
            "ada_b_d": ("ada_bT", [128, DEPTH * 48]),
            "gmix_d": ("gmixT", [128, DEPTH * 8]),
            "gffn_d": ("gffnT", [128, DEPTH * 8]),
            "gfin_d": ("gfinT", [128, 8]),
            "wqkv_d": ("wqkv", [D, 1536]),
            "wqkr_d": ("wqk_rot", [D, 1280]),
            "wo_d": ("wo", [D, D]),
            "wkd_d": ("wkd", [D, 512]),
            "wkrd_d": ("wkrd", [D, 512]),
            "wvd_d": ("wvd", [D, 512]),
            "sink_d": ("sinkT", [128, 8]),
            "sgu_win_d": ("sgu_w_in", [D, 2 * D]),
            "sgu_lng_d": ("sgu_lng_b", [128, D]),
            "sgu_ws_d": ("sgu_w_s", [8, 128, 128]),
            "sgu_bs_d": ("sgu_bs_b", [128, 8 * 128]),
            "sgu_wout_d": ("sgu_w_out", [D, D]),
            "sc_win_d": ("sc_w_in", [D, 3 * D]),
            "sc_conv_d": ("sc_convT", [128, 8 * 3]),
            "sc_wout_d": ("sc_w_out", [D, D]),
            "fn_wout_d": ("fn_w_out", [D, D]),
            "wup_d": ("ffn_w_up", [DEPTH, D, 2 * D_FF]),
            "fconv_d": ("ffn_convT", [128, DEPTH * 44 * 3]),
            "wdn_d": ("ffn_w_down", [DEPTH, D_FF, D]),
            "ident_d": ("ident", [128, 128]),
            "ropec_d": ("rope_cos", [128, 768]),
            "ropes_d": ("rope_sin", [128, 768]),
            "masks_d": ("masks", [128, 512]),
            "sel_d": ("selmask", [128, 8 * 16]),
            "dftc_d": ("dft_c", [128, 256]),
            "dftp_d": ("dft_tp", [256, 512]),
            "dfts_d": ("dft_ts", [2048, 1024]),
        }
        yp_d = self.outp("yp", [1024, D])
        ys_d = self.outp("ys", [512, D])
        nk_d = self.outp("nk", [1024, 256])
        nv_d = self.outp("nv", [1024, 256])
        class LazyD(dict):
            def __missing__(d_self, key):
                name, shape = specs[key]
                ap = self.inp(name, shape)
                d_self[key] = ap
                return ap
        self.dram = LazyD(yp_d=yp_d, ys_d=ys_d, nk_d=nk_d, nv_d=nv_d)
        d = self.dram

        self.x = self.sb("x", [128, NCH, NTOK], F32)
        self.h = self.sb("h", [128, NCH, HW], BF16)
        self.wring = self.sb("wring", [128, NSLOT, WSLOT], BF16)
        self.U = self.sb("U", [128, 34816], BF16)
        self.ps = nc.alloc_psum_tensor("ps", [128, 8, 512], F32)
        self.rotA = [0, 1, 2, 3, 4, 5]
        self.rotB = [6, 7]
        self.iA = self.iB = 0
        self.wi = 0
        ps = self.ps

        ident = self.load_const("ident", d["ident_d"], [128, 128])
        self.ident = ident
        cT = self.load_const("cT", d["cT_d"], [128, 16])
        ada_b = self.load_const("ada_b", d["ada_b_d"], [128, DEPTH * 48])
        gmix = self.load_const("gmix", d["gmix_d"], [128, DEPTH * 8])
        gffn = self.load_const("gffn", d["gffn_d"], [128, DEPTH * 8])
        gfin = self.load_const("gfin", d["gfin_d"], [128, 8])
        self.fconv = self.load_const("fconv", d["fconv_d"], [128, DEPTH * 44 * 3])
        self.sel = self.load_const("sel", d["sel_d"], [128, 128])

        self.onesm = self.sb("onesm", [128, 128], BF16)
        p.op("dve", lambda e: e.memset(self.onesm[:, :], 1.0 / 1024.0), writes=["onesm"])
        self.epsb = self.sb("epsb", [128, 1], F32)
        p.op("dve", lambda e: e.memset(self.epsb[:, :], EPS), writes=["epsb"])
        self.zerob = self.sb("zerob", [128, 1], F32)
        p.op("dve", lambda e: e.memset(self.zerob[:, :], 0.0), writes=["zerob"])
        p.op("dve", lambda e: e.memset(self.h[:, :, :], 0.0),
             writes=[("h", c, b) for c in range(NCH) for b in range(NBLK)] + ["hhalo"])

        self.sq = self.sb("sq", [128, NCH, 512], BF16)
        self.rstd = self.sb("rstd", [128, 2, 512], F32)
        self.tmpn = self.sb("tmpn", [128, 2, 512], F32)
        self.ntmp = 0
        self.tc = self.sb("tc", [128, 2, 3, 260], F32)
        self.itc = 0
        self.stage = self.U[:, 0:4096].bitcast(F32).rearrange("p (s d) -> p s d", s=2)
        self.istage = 0

        self.s_bf = self.sb("s_bf", [128, 16], BF16)
        p.op("act", lambda e: e.activation(out=self.s_bf[:, :], in_=cT[:, :], func=AF.Silu),
             reads=["cT"], writes=["s_bf"])
        self.mod = self.sb("mod", [128, DEPTH, 2, 48], F32)
        self.Amix = self.sb("Amix", [128, DEPTH, 2, 8], F32)
        self.Affn = self.sb("Affn", [128, DEPTH, 2, 8], F32)
        self.ada_b, self.gmix, self.gffn, self.gfin = ada_b, gmix, gffn, gfin

        self.load_x()
        for st in self.ada_steps(0):
            st()
        mixers = [self.mixer_attn, self.mixer_sgu, self.mixer_sconv, self.mixer_fnet]
        for l in range(NL):
            pending = list(self.ada_steps(l + 1)) if l + 1 < NL else []
            self.kind = cfg.get("mixsel", (0, 1, 2, 3))[l]
            self.norm_mod(l, "mix")
            if cfg.get("mix", (1, 1, 1, 1))[l]:
                mixers[self.kind](l)
            self.norm_mod(l, "ffn")
            if cfg.get("ffn", 1):
                self.ffn(l, pending)
            else:
                for st in pending:
                    st()
        self.final_norm()
        p.final_wait("sp", self.out_toks)
        p.emit()
        return nc

    def load_x(self):
        p, ps, x = self.p, self.ps, self.x
        d = self.dram
        srcs = [(d["xp_d"], i, i * 128) for i in range(8)] + [(d["xs_d"], i, 1024 + i * 128) for i in range(4)]
        for (src, i, col) in srcs:
            st = self.istage % 2
            self.istage += 1
            p.dma("sp", f"dst{st}", out=self.stage[:, st, :], in_=src[i * 128:(i + 1) * 128, :],
                  writes=[("stage", st)])
            for half in range(2):
                bk = self.bankA()
                def tr(e, st=st, half=half, bk=bk):
                    last = None
                    for cc in range(4):
                        c = half * 4 + cc
                        last = e.transpose(ps[:, bk, cc * 128:(cc + 1) * 128],
                                           self.stage[:, st, c * 128:(c + 1) * 128], self.ident[:, :])
                    return last
                p.op("pe", tr, reads=[("stage", st), "ident"], writes=[("ps", bk)])
                blk = col // 256
                p.op("dve" if half == 0 else "act",
                     (lambda e, bk=bk, half=half, col=col:
                      e.tensor_copy(out=x[:, half * 4:half * 4 + 4, col:col + 128],
                                    in_=ps[:, bk, :].rearrange("p (c t) -> p c t", c=4)))
                     if half == 0 else
                     (lambda e, bk=bk, half=half, col=col:
                      e.activation(out=x[:, half * 4:half * 4 + 4, col:col + 128],
                                   in_=ps[:, bk, :].rearrange("p (c t) -> p c t", c=4), func=AF.Copy)),
                     reads=[("ps", bk)], writes=[("x", half * 4 + cc, blk, col % 256) for cc in range(4)])

    def xkeys(self, cs, bs):
        return [("x", c, b, o) for c in cs for b in bs for o in (0, 128)]

    def ada_steps(self, l):
        p, ps = self.p, self.ps
        d = self.dram
        steps = []
        bk = 7
        for jb in range(12):
            def step(jb=jb):
                s, (wv,) = self.wload([(d["ada_w_d"][l][:, jb * 512:(jb + 1) * 512], 8, 512)])
                def mm(e, wv=wv, jb=jb):
                    last = None
                    for jj in range(4):
                        j = jb * 4 + jj
                        for kc in range(8):
                            last = e.matmul(ps[:, bk, j * 2:j * 2 + 2], lhsT=wv[:, kc, jj * 128:(jj + 1) * 128],
                                            rhs=self.s_bf[:, kc * 2:kc * 2 + 2], start=(kc == 0), stop=(kc == 7))
                    return last
                p.op("pe", mm, reads=[("w", s), "s_bf"], writes=[("modps", jb)] + ([("ps", bk)] if jb == 0 else []))
            steps.append(step)

        def fin():
            modl = self.mod[:, l, :, :]
            for v in range(2):
                p.op("dve", lambda e, v=v: e.tensor_tensor(
                    out=self.mod[:, l, v, :], in0=ps[:, bk, 0:96].rearrange("p (j v) -> p j v", v=2)[:, :, v],
                    in1=self.ada_b[:, l * 48:(l + 1) * 48], op=ALU.add),
                    reads=[("modps", jb) for jb in range(12)] + ["ada_b"], writes=[("mod", l, v)])
            p.res[("ps", bk)] = [("dve", p.cnt["dve"]), []]
            for v in range(2):
                p.op("dve", lambda e, v=v: e.scalar_tensor_tensor(
                    out=self.Amix[:, l, v, :], in0=self.mod[:, l, v, 8:16], scalar=1.0,
                    in1=self.gmix[:, l * 8:(l + 1) * 8], op0=ALU.add, op1=ALU.mult),
                    reads=[("mod", l, v), "gmix"], writes=[("A", l, v, 0)])
                p.op("dve", lambda e, v=v: e.scalar_tensor_tensor(
                    out=self.Affn[:, l, v, :], in0=self.mod[:, l, v, 32:40], scalar=1.0,
                    in1=self.gffn[:, l * 8:(l + 1) * 8], op0=ALU.add, op1=ALU.mult),
                    reads=[("mod", l, v), "gffn"], writes=[("A", l, v, 1)])
        steps.append(fin)
        return steps

    def tile_cols(self, T):
        return T * 512

    def norm_mod(self, l, which):
        p, ps, x, h = self.p, self.ps, self.x, self.h
        sub = 0 if which == "mix" else 1
        A = self.Amix if sub == 0 else self.Affn
        shoff = 0 if sub == 0 else 24
        for T in self.cfg.get("nmT", (2, 0, 1)):
            v = 1 if T == 2 else 0
            col = T * 512
            bs = (2 * T, 2 * T + 1)
            rstd = self.rms_stats(T)
            k = self.ntmp % 2
            for c in range(self.cfg.get("nmC", NCH)):
                tk = self.ntmp % 2
                self.ntmp += 1
                p.op("dve", lambda e, c=c, tk=tk, col=col, v=v, rstd=rstd: e.scalar_tensor_tensor(
                    out=self.tmpn[:, tk, :], in0=x[:, c, col:col + 512], scalar=A[:, l, v, c:c + 1],
                    in1=rstd, op0=ALU.mult, op1=ALU.mult),
                    reads=self.xkeys([c], bs) + [("A", l, v, sub), ("rstd", T % 2)], writes=[("tmpn", tk)])
                if T == 2:
                    outap = h[:, c, 1033:1545].rearrange("p (b t) -> p b t", b=2)
                else:
                    outap = h[:, c, hcol(2 * T):hcol(2 * T) + 516].rearrange("p (b t) -> p b t", b=2)[:, :, 1:257]
                inap = self.tmpn[:, tk, :].rearrange("p (b t) -> p b t", b=2)
                if self.cfg.get("nmACT", 1):
                  p.op("act", lambda e, outap=outap, inap=inap, c=c, v=v: e.activation(
                    out=outap, in_=inap, func=AF.Identity,
                    bias=self.mod[:, l, v, shoff + c:shoff + c + 1], scale=1.0),
                    reads=[("tmpn", tk), ("mod", l, v)], writes=[("h", c, b) for b in bs])
            if T == 2 and self.cfg.get("halo", 1) and (which == "ffn" or self.kind == 2):
                self.halo_exchange()

    def rms_stats(self, T):
        p, ps, x = self.p, self.ps, self.x
        col = T * 512
        bs = (2 * T, 2 * T + 1)
        p.op("act", lambda e: e.activation(out=self.sq[:, :, :], in_=x[:, :, col:col + 512], func=AF.Square),
             reads=self.xkeys(range(NCH), bs), writes=["sq"])
        bk = self.bankB()
        def mm(e):
            last = None
            for c in range(NCH):
                last = e.matmul(ps[:, bk, :], lhsT=self.onesm[:, :], rhs=self.sq[:, c, :],
                                start=(c == 0), stop=(c == NCH - 1))
            return last
        p.op("pe", mm, reads=["sq", "onesm"], writes=[("ps", bk)])
        r = self.rstd[:, T % 2, :]
        p.op("act", lambda e: e.activation(out=r, in_=ps[:, bk, :], func=AF.Sqrt, bias=self.epsb[:, 0:1], scale=1.0),
             reads=[("ps", bk), "epsb"], writes=[("rstd", T % 2)])
        p.op("dve", lambda e: e.reciprocal(out=r, in_=r), reads=[("rstd", T % 2)], writes=[("rstd", T % 2)])
        return r

    def halo_exchange(self):
        p, h, nc = self.p, self.h, self.nc
        k = self.ncc
        self.ncc += 1
        cc_in = nc.dram_tensor(f"cc_in{k}", [128, 16], F32)
        cc_out = nc.dram_tensor(f"cc_out{k}", [8 * 128, 16], F32)
        kx = k
        if not hasattr(self, "_hx"):
            self._hx = (self.sb("hb", [128, 8, 2], F32), self.sb("ga", [128, 8, 16], F32), self.sb("red", [128, 16], F32))
        hb, ga, red = self._hx
        kk, k = k, 0
        hkeys = [("h", c, b) for c in range(NCH) for b in (4, 5)]
        p.op("dve", lambda e: e.tensor_copy(out=hb[:, :, 0], in_=h[:, :, 1033]), reads=hkeys, writes=[("hb", k, 0)])
        p.op("dve", lambda e: e.tensor_copy(out=hb[:, :, 1], in_=h[:, :, 1544]), reads=hkeys, writes=[("hb", k, 1)])
        p.dma("sp", f"dcci{kx}", out=cc_in.ap(), in_=hb[:, :, :].rearrange("p c v -> p (c v)"),
              reads=[("hb", k, 0), ("hb", k, 1)], writes=[("cc_in", kx)])

        def cc(eng, dsem):
            eng.collective_compute("AllGather", ALU.bypass, replica_groups=[list(range(8))],
                                   ins=[cc_in.ap().opt()], outs=[cc_out.ap().opt()]).then_inc(dsem, 1)
        p.dma("pool", f"dcc{kx}", fn=cc, inc=1, reads=[("cc_in", kx)], writes=[("cc_out", kx)])
        p.dma("sp", f"dcco{kx}", out=ga[:, :, :], in_=cc_out.ap().rearrange("(r p) f -> p r f", p=128),
              reads=[("cc_out", kx)], writes=[("ga", k)])
        p.op("dve", lambda e: e.tensor_tensor(out=ga[:, :, :], in0=ga[:, :, :],
                                              in1=self.sel[:, :].rearrange("p (r f) -> p r f", r=8), op=ALU.mult),
             reads=[("ga", k), "sel"], writes=[("ga", k)])
        p.op("dve", lambda e: e.tensor_reduce(out=red[:, :], in_=ga[:, :, :].rearrange("p r f -> p f r"),
                                              axis=AX.X, op=ALU.add),
             reads=[("ga", k)], writes=[("red", k)])
        rv = red[:, :].rearrange("p (c v) -> p c v", v=2)
        p.op("dve", lambda e: e.tensor_copy(out=h[:, :, 1032], in_=rv[:, :, 1]), reads=[("red", k)], writes=["hhaloL"])
        p.op("dve", lambda e: e.tensor_copy(out=h[:, :, 1545], in_=rv[:, :, 0]), reads=[("red", k)], writes=["hhaloR"])

    def hkeys(self, b):
        ks = [("h", c, b) for c in range(NCH)]
        if b >= 4:
            ks += ["hhaloL", "hhaloR"] + [("h", c, 9 - b) for c in range(NCH)]
        return ks

    def ffn(self, l, pending):
        p, ps, x, h = self.p, self.ps, self.x, self.h
        d = self.dram
        act = self.U[:, 0:NJ * NTOK].rearrange("p (j t) -> p j t", j=NJ)
        fc = self.fconv[:, l * 132:(l + 1) * 132].rearrange("p (j k) -> p j k", k=3)
        wup = d["wup_d"][l]
        wdn = d["wdn_d"][l]
        pend = list(pending)
        for jb in range(11):
            s, (wg, wu) = self.wload([(wup[:, jb * 256:(jb + 1) * 256], 8, 256),
                                      (wup[:, D_FF + jb * 256:D_FF + (jb + 1) * 256], 8, 256)])
            for jj in range(2):
                j = 2 * jb + jj
                for b in range(NBLK):
                    hc = hcol(b)
                    bg, bu = self.bankA(), self.bankA()
                    def mm(e, wg=wg, wu=wu, jj=jj, hc=hc, bg=bg, bu=bu):
                        last = None
                        for (w_, bk) in ((wg, bg), (wu, bu)):
                            for kc in range(8):
                                last = e.matmul(ps[:, bk, 0:258], lhsT=w_[:, kc, jj * 128:(jj + 1) * 128],
                                                rhs=h[:, kc, hc:hc + 258], start=(kc == 0), stop=(kc == 7))
                        return last
                    p.op("pe", mm, reads=[("w", s)] + self.hkeys(b), writes=[("ps", bg), ("ps", bu)])
                    it = self.itc % 2
                    self.itc += 1
                    tg, tu, sg = self.tc[:, it, 0, 0:256], self.tc[:, it, 1, 0:256], self.tc[:, it, 2, 0:256]
                    for (bk, t_, jc) in ((bg, tg, j), (bu, tu, NJ + j)):
                        p.op("act", lambda e, bk=bk, t_=t_, jc=jc: e.activation(
                            out=t_, in_=ps[:, bk, 0:256], func=AF.Copy, scale=fc[:, jc, 0:1]),
                            reads=[("ps", bk), "fconv"], writes=[("tc", it, 0 if t_ is tg else 1)])
                        for tap in (1, 2):
                            p.op("dve", lambda e, bk=bk, t_=t_, jc=jc, tap=tap: e.scalar_tensor_tensor(
                                out=t_, in0=ps[:, bk, tap:tap + 256], scalar=fc[:, jc, tap:tap + 1], in1=t_,
                                op0=ALU.mult, op1=ALU.add),
                                reads=[("ps", bk), "fconv", ("tc", it, 0 if t_ is tg else 1)],
                                writes=[("tc", it, 0 if t_ is tg else 1)] + ([("psdone", bk)] if tap == 2 else []))
                    p.op("act", lambda e, tg=tg, sg=sg: e.activation(out=sg, in_=tg, func=AF.Silu),
                         reads=[("tc", it, 0)], writes=[("tc", it, 2)])
                    p.op("dve", lambda e, sg=sg, tu=tu, j=j, b=b: e.tensor_tensor(
                        out=act[:, j, b * 256:(b + 1) * 256], in0=sg, in1=tu, op=ALU.mult),
                        reads=[("tc", it, 2), ("tc", it, 1)], writes=[("act", j, b)])
            if pend:
                pend.pop(0)()
        for oc in range(NCH):
            s, (wd,) = self.wload([(wdn[:, oc * 128:(oc + 1) * 128], NJ, 128)])
            for T in range(3):
                bk = self.bankA()
                v = 1 if T == 2 else 0
                def mm(e, wd=wd, T=T, bk=bk):
                    last = None
                    for j in range(NJ):
                        last = e.matmul(ps[:, bk, :], lhsT=wd[:, j, :], rhs=act[:, j, T * 512:(T + 1) * 512],
                                        start=(j == 0), stop=(j == NJ - 1))
                    return last
                ltok = p.op("pe", mm, reads=[("w", s)] + [("act", j, b) for j in range(NJ) for b in (2 * T, 2 * T + 1)],
                            writes=[("ps", bk)])
                p.res["Ubar"] = [ltok, []]
                p.op("dve", lambda e, oc=oc, T=T, bk=bk, v=v: e.scalar_tensor_tensor(
                    out=x[:, oc, T * 512:(T + 1) * 512], in0=ps[:, bk, :], scalar=self.mod[:, l, v, 40 + oc:41 + oc],
                    in1=x[:, oc, T * 512:(T + 1) * 512], op0=ALU.mult, op1=ALU.add),
                    reads=[("ps", bk), ("mod", l, v)] + self.xkeys([oc], (2 * T, 2 * T + 1)),
                    writes=self.xkeys([oc], (2 * T, 2 * T + 1)))
            if pend:
                pend.pop(0)()
        while pend:
            pend.pop(0)()

    def out_proj(self, l, w_dram, src, src_keys):
        p, ps, x = self.p, self.ps, self.x
        for half in range(2):
            s, (wv,) = self.wload([(w_dram[:, half * 512:(half + 1) * 512], 8, 512)])
            for oo in range(4):
                oc = half * 4 + oo
                for T in range(3):
                    bk = self.bankA()
                    v = 1 if T == 2 else 0
                    def mm(e, wv=wv, oo=oo, T=T, bk=bk):
                        last = None
                        for kc in range(8):
                            last = e.matmul(ps[:, bk, :], lhsT=wv[:, kc, oo * 128:(oo + 1) * 128],
                                            rhs=src[:, kc, T * 512:(T + 1) * 512], start=(kc == 0), stop=(kc == 7))
                        return last
                    ltok = p.op("pe", mm, reads=[("w", s)] + src_keys(T), writes=[("ps", bk)])
                    p.res["Ubar"] = [ltok, []]
                    p.op("dve", lambda e, oc=oc, T=T, bk=bk, v=v: e.scalar_tensor_tensor(
                        out=x[:, oc, T * 512:(T + 1) * 512], in0=ps[:, bk, :],
                        scalar=self.mod[:, l, v, 16 + oc:17 + oc],
                        in1=x[:, oc, T * 512:(T + 1) * 512], op0=ALU.mult, op1=ALU.add),
                        reads=[("ps", bk), ("mod", l, v)] + self.xkeys([oc], (2 * T, 2 * T + 1)),
                        writes=self.xkeys([oc], (2 * T, 2 * T + 1)))

    def tokc(self, b):
        return hcol(b) + 1

    def gkeys(self, T):
        return [("gt", c, b) for c in range(NCH) for b in (2 * T, 2 * T + 1)]

    def mixer_sconv(self, l):
        p, ps, h, d = self.p, self.ps, self.h, self.dram
        gated = self.U[:, 0:NCH * NTOK].rearrange("p (c t) -> p c t", c=NCH)
        scw = self.load_const("scw", d["sc_conv_d"], [128, 24])
        win = d["sc_win_d"]
        for c in range(NCH):
            s, (wb, wc, wx) = self.wload([(win[:, c * 128:(c + 1) * 128], 8, 128),
                                          (win[:, 1024 + c * 128:1024 + (c + 1) * 128], 8, 128),
                                          (win[:, 2048 + c * 128:2048 + (c + 1) * 128], 8, 128)])
            for b in range(NBLK):
                hc = hcol(b)
                kb, kc_, kx = self.bankA(), self.bankA(), self.bankA()
                def mm(e, wb=wb, wc=wc, wx=wx, hc=hc, kb=kb, kc_=kc_, kx=kx):
                    last = None
                    for (w_, bk) in ((wb, kb), (wc, kc_), (wx, kx)):
                        for kc in range(8):
                            last = e.matmul(ps[:, bk, 0:258], lhsT=w_[:, kc, :], rhs=h[:, kc, hc:hc + 258],
                                            start=(kc == 0), stop=(kc == 7))
                    return last
                p.op("pe", mm, reads=[("w", s)] + self.hkeys(b), writes=[("ps", kb), ("ps", kc_), ("ps", kx)])
                it = self.itc % 2
                self.itc += 1
                t0, t1, t2 = self.tc[:, it, 0, :], self.tc[:, it, 1, :], self.tc[:, it, 2, :]
                p.op("act", lambda e, t0=t0, kc_=kc_: e.activation(out=t0[:, 0:258], in_=ps[:, kc_, 0:258], func=AF.Copy),
                     reads=[("ps", kc_)], writes=[("tc", it, 0)])
                p.op("dve", lambda e, t0=t0, t1=t1, kx=kx: e.tensor_tensor(out=t1[:, 0:258], in0=ps[:, kx, 0:258],
                                                                          in1=t0[:, 0:258], op=ALU.mult),
                     reads=[("ps", kx), ("tc", it, 0)], writes=[("tc", it, 1)])
                p.op("act", lambda e, t1=t1, t2=t2, c=c: e.activation(out=t2[:, 0:256], in_=t1[:, 0:256], func=AF.Copy,
                                                                      scale=scw[:, c * 3:c * 3 + 1]),
                     reads=[("tc", it, 1), "scw"], writes=[("tc", it, 2)])
                for tap in (1, 2):
                    p.op("dve", lambda e, t1=t1, t2=t2, c=c, tap=tap: e.scalar_tensor_tensor(
                        out=t2[:, 0:256], in0=t1[:, tap:tap + 256], scalar=scw[:, c * 3 + tap:c * 3 + tap + 1],
                        in1=t2[:, 0:256], op0=ALU.mult, op1=ALU.add),
                        reads=[("tc", it, 1), ("tc", it, 2), "scw"], writes=[("tc", it, 2)])
                p.op("dve", lambda e, t2=t2, kb=kb, c=c, b=b: e.tensor_tensor(
                    out=gated[:, c, b * 256:(b + 1) * 256], in0=ps[:, kb, 1:257], in1=t2[:, 0:256], op=ALU.mult),
                    reads=[("ps", kb), ("tc", it, 2)], writes=[("gt", c, b)])
        self.out_proj(l, d["sc_wout_d"], gated, self.gkeys)

    def mixer_sgu(self, l):
        p, ps, h, d, U = self.p, self.ps, self.h, self.dram, self.U
        vn = U[:, 0:12288].rearrange("p (t c) -> p t c", t=12)
        gated = U[:, 12288:24576].rearrange("p (c t) -> p c t", c=NCH)
        lng = U[:, 24576:26624].bitcast(F32)
        bsb = U[:, 26624:28672].bitcast(F32)
        wsT = U[:, 28672:29696].rearrange("p (g r) -> p g r", g=8)
        wsn = U[:, 29696:31744].bitcast(F32).rearrange("p (g r) -> p g r", g=8)
        vtmp = U[:, 31744:33792].bitcast(F32)
        sqt = self.tmpn[:, :, :].rearrange("p a b -> p (a b)")
        st = self.sb("sgu_st", [128, 12, 8], F32)
        p.dma("sp", "dc_lng", out=lng, in_=d["sgu_lng_d"], reads=["Ubar"], writes=["lng"])
        p.dma("sp", "dc_bsb", out=bsb, in_=d["sgu_bs_d"], reads=["Ubar"], writes=["bsb"])
        p.dma("sp", "dc_wsn", out=wsn, in_=d["sgu_ws_d"].rearrange("g p r -> p g r"), reads=["Ubar"], writes=["wsn"])
        for half in range(2):
            bk = self.bankA()
            def tr(e, half=half, bk=bk):
                last = None
                for gg in range(4):
                    last = e.transpose(ps[:, bk, gg * 128:(gg + 1) * 128], wsn[:, half * 4 + gg, :], self.ident[:, :])
                return last
            p.op("pe", tr, reads=["wsn", "ident"], writes=[("ps", bk)])
            p.op("dve", lambda e, half=half, bk=bk: e.tensor_copy(
                out=wsT[:, half * 4:half * 4 + 4, :], in_=ps[:, bk, :].rearrange("p (g r) -> p g r", g=4)),
                reads=[("ps", bk)], writes=[("wsT", half)])
        win = d["sgu_win_d"]
        s1, (wv0,) = self.wload([(win[:, 1024:1536], 8, 512)])
        s2, (wv1,) = self.wload([(win[:, 1536:2048], 8, 512)])
        for tb in range(12):
            b, o = tb // 2, (tb % 2) * 128
            tcol = self.tokc(b) + o
            for half, (s_, wv) in enumerate(((s1, wv0), (s2, wv1))):
                bk = self.bankA()
                def mm(e, wv=wv, tcol=tcol, bk=bk):
                    last = None
                    for kc in range(8):
                        last = e.matmul(ps[:, bk, :], lhsT=h[:, kc, tcol:tcol + 128], rhs=wv[:, kc, :],
                                        start=(kc == 0), stop=(kc == 7))
                    return last
                p.op("pe", mm, reads=[("w", s_)] + self.hkeys(b), writes=[("ps", bk)])
                p.op("act", lambda e, half=half, bk=bk: e.activation(out=vtmp[:, half * 512:(half + 1) * 512],
                                                                     in_=ps[:, bk, :], func=AF.Gelu_apprx_tanh),
                     reads=[("ps", bk)], writes=[("vtmp", half)])
            vk = [("vtmp", 0), ("vtmp", 1)]
            sv = st[:, tb, :]
            p.op("dve", lambda e, sv=sv: e.tensor_reduce(out=sv[:, 0:1], in_=vtmp, axis=AX.X, op=ALU.add),
                 reads=vk, writes=[("st", tb, 0)])
            p.op("dve", lambda e: e.tensor_tensor(out=sqt, in0=vtmp, in1=vtmp, op=ALU.mult),
                 reads=vk, writes=[("tmpn", 0), ("tmpn", 1)])
            p.op("dve", lambda e, sv=sv: e.tensor_reduce(out=sv[:, 1:2], in_=sqt, axis=AX.X, op=ALU.add),
                 reads=[("tmpn", 0), ("tmpn", 1)], writes=[("st", tb, 1)])
            p.op("dve", lambda e, sv=sv: e.tensor_scalar(out=sv[:, 2:3], in0=sv[:, 0:1], scalar1=1.0 / 1024.0, scalar2=None,
                                                         op0=ALU.mult),
                 reads=[("st", tb, 0)], writes=[("st", tb, 2)])
            p.op("dve", lambda e, sv=sv: e.tensor_tensor(out=sv[:, 3:4], in0=sv[:, 2:3], in1=sv[:, 2:3], op=ALU.mult),
                 reads=[("st", tb, 2)], writes=[("st", tb, 3)])
            p.op("dve", lambda e, sv=sv: e.scalar_tensor_tensor(out=sv[:, 4:5], in0=sv[:, 1:2], scalar=1.0 / 1024.0,
                                                                in1=sv[:, 3:4], op0=ALU.mult, op1=ALU.subtract),
                 reads=[("st", tb, 1), ("st", tb, 3)], writes=[("st", tb, 4)])
            p.op("act", lambda e, sv=sv: e.activation(out=sv[:, 5:6], in_=sv[:, 4:5], func=AF.Sqrt,
                                                      bias=self.epsb[:, 0:1], scale=1.0),
                 reads=[("st", tb, 4), "epsb"], writes=[("st", tb, 5)])
            p.op("dve", lambda e, sv=sv: e.reciprocal(out=sv[:, 6:7], in_=sv[:, 5:6]),
                 reads=[("st", tb, 5)], writes=[("st", tb, 6)])
            p.op("dve", lambda e, sv=sv: e.tensor_scalar(out=sqt, in0=vtmp, scalar1=sv[:, 2:3], scalar2=sv[:, 6:7],
                                                         op0=ALU.subtract, op1=ALU.mult),
                 reads=vk + [("st", tb, 2), ("st", tb, 6), ("tmpn", 0), ("tmpn", 1)], writes=[("tmpn", 0), ("tmpn", 1)])
            p.op("dve", lambda e, tb=tb: e.tensor_tensor(out=vn[:, tb, :], in0=sqt, in1=lng, op=ALU.mult),
                 reads=[("tmpn", 0), ("tmpn", 1), "lng"], writes=[("vn", tb)])
        for half in range(2):
            s, (wu,) = self.wload([(win[:, half * 512:(half + 1) * 512], 8, 512)])
            for gg in range(4):
                gi = half * 4 + gg
                for b in range(NBLK):
                    tc0 = self.tokc(b)
                    bu, bm = self.bankA(), self.bankA()
                    def mm(e, wu=wu, gg=gg, gi=gi, b=b, tc0=tc0, bu=bu, bm=bm):
                        last = None
                        for kc in range(8):
                            last = e.matmul(ps[:, bu, 0:256], lhsT=wu[:, kc, gg * 128:(gg + 1) * 128],
                                            rhs=h[:, kc, tc0:tc0 + 256], start=(kc == 0), stop=(kc == 7))
                        for o in range(2):
                            last = e.matmul(ps[:, bm, o * 128:(o + 1) * 128], lhsT=vn[:, 2 * b + o, gi * 128:(gi + 1) * 128],
                                            rhs=wsT[:, gi, :], start=True, stop=True)
                        return last
                    p.op("pe", mm, reads=[("w", s), ("vn", 2 * b), ("vn", 2 * b + 1), ("wsT", gi // 4)] + self.hkeys(b),
                         writes=[("ps", bu), ("ps", bm)])
                    it = self.itc % 2
                    self.itc += 1
                    t0, t1 = self.tc[:, it, 0, :], self.tc[:, it, 1, :]
                    p.op("act", lambda e, t0=t0, bu=bu: e.activation(out=t0[:, 0:256], in_=ps[:, bu, 0:256],
                                                                     func=AF.Gelu_apprx_tanh),
                         reads=[("ps", bu)], writes=[("tc", it, 0)])
                    for o in range(2):
                        p.op("dve", lambda e, t1=t1, bm=bm, gi=gi, o=o: e.tensor_tensor(
                            out=t1[:, o * 128:(o + 1) * 128], in0=ps[:, bm, o * 128:(o + 1) * 128],
                            in1=bsb[:, gi * 128:(gi + 1) * 128], op=ALU.add),
                            reads=[("ps", bm), "bsb"] + ([("tc", it, 1)] if o else []), writes=[("tc", it, 1)])
                    p.op("dve", lambda e, t0=t0, t1=t1, gi=gi, b=b: e.tensor_tensor(
                        out=gated[:, gi, b * 256:(b + 1) * 256], in0=t0[:, 0:256], in1=t1[:, 0:256], op=ALU.mult),
                        reads=[("tc", it, 0), ("tc", it, 1)], writes=[("gt", gi, b)])
        self.out_proj(l, d["sgu_wout_d"], gated, self.gkeys)

    def mixer_fnet(self, l):
        p, ps, h, d, U, nc = self.p, self.ps, self.h, self.dram, self.U, self.nc
        FT = U[:, 0:12288].rearrange("p (c t) -> p c t", c=NCH)
        hsa = U[:, 12288:28672].rearrange("p (c t) -> p c t", c=NCH)
        ABs = U[:, 28672:32768].rearrange("p (g t n) -> p g t n", g=4, t=4)
        ABp = U[:, 32768:33792].rearrange("p (k t n) -> p k t n", k=2, t=2)
        dftc = self.load_const("dftc", d["dftc_d"], [128, 256], BF16, cast=True)
        dftp = self.sb("dftp", [128, 2, 512], BF16)
        p.dma("pool", "dc_dftp", out=dftp[:, :, :], in_=d["dftp_d"].rearrange("(t p) n -> p t n", p=128), writes=["dftp"])
        cc_in = nc.dram_tensor("fn_cc_in", [128, 4096], BF16)
        cc_out = nc.dram_tensor("fn_cc_out", [512, 4096], BF16)
        hk = [("h", c, b) for c in range(NCH) for b in (4, 5)]
        p.dma("sp", "dfcci", out=cc_in.ap().rearrange("p (c t) -> p c t", c=8), in_=h[:, :, 1033:1545],
              reads=hk, writes=["fn_cc_in"])

        def cc(eng, dsem):
            eng.collective_compute("AllGather", ALU.bypass, replica_groups=[[0, 1, 2, 3], [4, 5, 6, 7]],
                                   ins=[cc_in.ap().opt()], outs=[cc_out.ap().opt()]).then_inc(dsem, 1)
        p.dma("pool", "dfcc", fn=cc, inc=1, reads=["fn_cc_in"], writes=["fn_cc_out"])
        p.dma("sp", "dfcco", out=hsa.rearrange("p c (r t) -> p c r t", r=4),
              in_=cc_out.ap().rearrange("(r p) (c t) -> p c r t", p=128, c=8),
              reads=["fn_cc_out", "Ubar"], writes=["hsa"])
        ib = 0
        for sq in range(4):
            tc0 = self.tokc(sq)
            for gi in range(NCH):
                k = ib % 2
                ib += 1
                bk = self.bankA()
                def mm(e, gi=gi, tc0=tc0, bk=bk):
                    last = None
                    for t in range(2):
                        last = e.matmul(ps[:, bk, t * 256:(t + 1) * 256], lhsT=h[:, gi, tc0 + t * 128:tc0 + (t + 1) * 128],
                                        rhs=dftc[:, :], start=True, stop=True)
                    return last
                p.op("pe", mm, reads=["dftc"] + self.hkeys(sq), writes=[("ps", bk)])
                p.op("act", lambda e, k=k, bk=bk: e.activation(out=ABp[:, k, :, :],
                                                               in_=ps[:, bk, :].rearrange("p (t n) -> p t n", t=2), func=AF.Copy),
                     reads=[("ps", bk)], writes=[("ABp", k)])
                bo = self.bankA()
                def mm2(e, k=k, bo=bo):
                    last = None
                    i = 0
                    for t in range(2):
                        for ab in range(2):
                            last = e.matmul(ps[:, bo, 0:256], lhsT=ABp[:, k, t, ab * 128:(ab + 1) * 128],
                                            rhs=dftp[:, t, ab * 256:(ab + 1) * 256], start=(i == 0), stop=(i == 3))
                            i += 1
                    return last
                p.op("pe", mm2, reads=[("ABp", k), "dftp"], writes=[("ps", bo)])
                p.op("dve", lambda e, gi=gi, sq=sq, bo=bo: e.tensor_copy(out=FT[:, gi, sq * 256:(sq + 1) * 256],
                                                                         in_=ps[:, bo, 0:256]),
                     reads=[("ps", bo)], writes=[("gt", gi, sq)])
        saveA, saveiA = self.rotA, self.iA
        self.rotA, self.iA = [4, 5], 0
        dts = d["dfts_d"]
        for pas in range(2):
            for t4 in range(4):
                s, (tab,) = self.wload([(dts[t4 * 512:(t4 + 1) * 512, :], 4, 1024)])
                for gq in range(4):
                    gi = pas * 4 + gq
                    for tp in range(2):
                        bk = self.bankA()
                        def mm(e, gi=gi, t4=t4, tp=tp, bk=bk):
                            last = None
                            for t in range(2):
                                tt = t4 * 4 + tp * 2 + t
                                last = e.matmul(ps[:, bk, t * 256:(t + 1) * 256], lhsT=hsa[:, gi, tt * 128:(tt + 1) * 128],
                                                rhs=dftc[:, :], start=True, stop=True)
                            return last
                        p.op("pe", mm, reads=["dftc", "hsa"], writes=[("ps", bk)])
                        p.op("act", lambda e, gq=gq, tp=tp, bk=bk: e.activation(
                            out=ABs[:, gq, tp * 2:tp * 2 + 2, :], in_=ps[:, bk, :].rearrange("p (t n) -> p t n", t=2),
                            func=AF.Copy), reads=[("ps", bk)], writes=[("ABs", gq, tp)])
                    def mm2(e, gq=gq, t4=t4, tab=tab):
                        last = None
                        i = 0
                        for t in range(4):
                            for ab in range(2):
                                last = e.matmul(ps[:, gq, :], lhsT=ABs[:, gq, t, ab * 128:(ab + 1) * 128],
                                                rhs=tab[:, t, ab * 512:(ab + 1) * 512],
                                                start=(t4 == 0 and i == 0), stop=(t4 == 3 and i == 7))
                                i += 1
                        return last
                    p.op("pe", mm2, reads=[("ABs", gq, 0), ("ABs", gq, 1), ("w", s)], writes=[("ps", gq)])
            for gq in range(4):
                gi = pas * 4 + gq
                p.op("dve" if gq % 2 == 0 else "act",
                     (lambda e, gi=gi, gq=gq: e.tensor_copy(out=FT[:, gi, 1024:1536], in_=ps[:, gq, :])) if gq % 2 == 0 else
                     (lambda e, gi=gi, gq=gq: e.activation(out=FT[:, gi, 1024:1536], in_=ps[:, gq, :], func=AF.Copy)),
                     reads=[("ps", gq)], writes=[("gt", gi, 4), ("gt", gi, 5)])
        self.rotA, self.iA = saveA, saveiA
        self.out_proj(l, d["fn_wout_d"], FT, self.gkeys)

    def mixer_attn(self, l):
        p, ps, h, d, U, x = self.p, self.ps, self.h, self.dram, self.U, self.x
        OT = U[:, 0:12288].rearrange("p (c t) -> p c t", c=NCH)
        KT = U[:, 12288:16384].rearrange("p (g t) -> p g t", g=4)
        KTs = U[:, 16384:19456].rearrange("p (g t) -> p g t", g=4)
        KTc = U[:, 19456:21504].rearrange("p (g t) -> p g t", g=4)
        Vp = U[:, 21504:25600].rearrange("p (k g d) -> p k g d", k=8, g=4)
        Vs = U[:, 25600:28672].rearrange("p (k g d) -> p k g d", k=6, g=4)
        Vc = U[:, 28672:30720].rearrange("p (k g d) -> p k g d", k=4, g=4)
        QT = U[:, 30720:32256]
        PT = U[:, 32256:33792].rearrange("p (k t) -> p k t", k=3)
        xh = U[:, 4096:8192].bitcast(F32).rearrange("p (c t) -> p c t", c=NCH)
        hh = U[:, 8192:10240].rearrange("p (c t) -> p c t", c=NCH)
        ckd = U[:, 10240:12288].rearrange("p (k g d) -> p k g d", k=4, g=4)
        ropec = self.tmpn[:, :, :].rearrange("p a b -> p (a b)")
        ropes = self.rstd[:, :, :].rearrange("p a b -> p (a b)")
        sqf = self.sq[:, :, :].rearrange("p a b -> p (a b)").bitcast(F32)
        tA, tB, rvt = sqf[:, 0:768], sqf[:, 768:1536], sqf[:, 1536:2048]
        TMPK = ([("xh", a, b_) for a in range(2) for b_ in range(2)] + [("hh", c) for c in range(NCH)] +
                [("ckd", a_, b_) for a_ in range(2) for b_ in range(4)] + [("stage", 0), ("stage", 1), ("stage", 0, 0), ("stage", 0, 1), ("stage", 1, 0), ("stage", 1, 1)])
        RK = [("tmpn", 0), ("tmpn", 1), ("rstd", 0), ("rstd", 1)]

        ASTAGE = self.cfg.get("astage", 9)
        masks = self.load_const("masks", d["masks_d"], [128, 512], BF16, cast=True)
        sink = self.load_const("sink", d["sink_d"], [128, 8])
        esink = self.sb("esink", [128, 8], F32)
        p.op("act", lambda e: e.activation(out=esink[:, :], in_=sink[:, :], func=AF.Exp), reads=["sink"], writes=["esink"])
        identb = self.sb("identb", [128, 128], BF16)
        p.op("dve", lambda e: e.tensor_copy(out=identb[:, :], in_=self.ident[:, :]), reads=["ident"], writes=["identb"])
        onesb = self.sb("onesb", [128, 128], BF16)
        p.op("dve", lambda e: e.memset(onesb[:, :], 1.0), writes=["onesb"])

        for blk in range(2):
            st = self.istage % 2
            self.istage += 1
            p.dma("sp", f"dst{st}", out=self.stage[:, st, :], in_=d["xsh_d"][blk * 128:(blk + 1) * 128, :],
                  reads=["Ubar"], writes=[("stage", st), ("stage", st, 0), ("stage", st, 1)])
            for half in range(2):
                bk = self.bankA()
                def tr(e, st=st, half=half, bk=bk):
                    last = None
                    for cc in range(4):
                        c = half * 4 + cc
                        last = e.transpose(ps[:, bk, cc * 128:(cc + 1) * 128],
                                           self.stage[:, st, c * 128:(c + 1) * 128], self.ident[:, :])
                    return last
                p.op("pe", tr, reads=[("stage", st), "ident"], writes=[("ps", bk)])
                p.op("dve", lambda e, bk=bk, half=half, blk=blk: e.tensor_copy(
                    out=xh[:, half * 4:half * 4 + 4, blk * 128:(blk + 1) * 128],
                    in_=ps[:, bk, :].rearrange("p (c t) -> p c t", c=4)),
                    reads=[("ps", bk)], writes=[("xh", half, blk)])
        xhk = [("xh", a, b_) for a in range(2) for b_ in range(2)]
        p.op("act", lambda e: e.activation(out=self.sq[:, :, 0:256], in_=xh, func=AF.Square), reads=xhk, writes=["sq"])
        bk = self.bankB()
        def mmst(e, bk=bk):
            last = None
            for c in range(NCH):
                last = e.matmul(ps[:, bk, 0:256], lhsT=self.onesm[:, :], rhs=self.sq[:, c, 0:256],
                                start=(c == 0), stop=(c == NCH - 1))
            return last
        p.op("pe", mmst, reads=["sq", "onesm"], writes=[("ps", bk)])
        rh = self.rstd[:, 0, 0:256]
        p.op("act", lambda e, bk=bk: e.activation(out=rh, in_=ps[:, bk, 0:256], func=AF.Sqrt, bias=self.epsb[:, 0:1], scale=1.0),
             reads=[("ps", bk), "epsb"], writes=[("rstd", 0)])
        p.op("dve", lambda e: e.reciprocal(out=rh, in_=rh), reads=[("rstd", 0)], writes=[("rstd", 0)])
        for c in range(NCH):
            tk = c % 2
            p.op("dve", lambda e, c=c, tk=tk: e.scalar_tensor_tensor(
                out=self.tmpn[:, tk, 0:256], in0=xh[:, c, :], scalar=self.Amix[:, l, 1, c:c + 1], in1=rh,
                op0=ALU.mult, op1=ALU.mult), reads=xhk + [("A", l, 1, 0), ("rstd", 0)], writes=[("tmpn", tk)])
            p.op("act", lambda e, c=c, tk=tk: e.activation(
                out=hh[:, c, :].rearrange("p (b t) -> p b t", b=2),
                in_=self.tmpn[:, tk, 0:256].rearrange("p (b t) -> p b t", b=2), func=AF.Identity,
                bias=self.mod[:, l, 1, c:c + 1], scale=1.0), reads=[("tmpn", tk), ("mod", l, 1)], writes=[("hh", c)])
        hhk = [("hh", c) for c in range(NCH)]
        p.dma("sp", "dc_ropec", out=ropec[:, 0:768], in_=d["ropec_d"], writes=[("tmpn", 0), ("tmpn", 1)])
        p.dma("sp", "dc_ropes", out=ropes[:, 0:768], in_=d["ropes_d"], writes=[("rstd", 0), ("rstd", 1)])

        if ASTAGE < 0.2:
            return
        skv, (wkv,) = self.wload([(d["wqkv_d"][:, 1024:1536], 8, 512)])
        skd, (wkd,) = self.wload([(d["wkd_d"], 8, 512)])
        skr, (wkrd,) = self.wload([(d["wkrd_d"], 8, 512)])
        svd, (wvd,) = self.wload([(d["wvd_d"], 8, 512)])
        hs_keys = self.hkeys(4)
        for g in range(4):
            for b in range(4):
                tc0 = self.tokc(b)
                bk = self.bankA()
                def mm(e, g=g, tc0=tc0, bk=bk):
                    last = None
                    for kc in range(8):
                        last = e.matmul(ps[:, bk, 0:256], lhsT=wkd[:, kc, g * 128:(g + 1) * 128], rhs=h[:, kc, tc0:tc0 + 256],
                                        start=(kc == 0), stop=(kc == 7))
                    return last
                p.op("pe", mm, reads=[("w", skd)] + self.hkeys(b), writes=[("ps", bk)])
                eng = "act" if (g + b) % 2 else "dve"
                if eng == "act":
                    p.op("act", lambda e, g=g, b=b, bk=bk: e.activation(out=KT[:, g, b * 256:(b + 1) * 256], in_=ps[:, bk, 0:256],
                                                                        func=AF.Copy), reads=[("ps", bk)], writes=[("KT", g, b)])
                else:
                    p.op("dve", lambda e, g=g, b=b, bk=bk: e.tensor_copy(out=KT[:, g, b * 256:(b + 1) * 256], in_=ps[:, bk, 0:256]),
                         reads=[("ps", bk)], writes=[("KT", g, b)])
        if ASTAGE < 0.3:
            return
        for tb in range(8):
            b, o = tb // 2, (tb % 2) * 128
            tcol = self.tokc(b) + o
            bk = self.bankA()
            def mm(e, tcol=tcol, bk=bk):
                last = None
                for part in range(2):
                    for kc in range(8):
                        last = e.matmul(ps[:, bk, part * 256:(part + 1) * 256], lhsT=h[:, kc, tcol:tcol + 128],
                                        rhs=wkv[:, kc, part * 256:(part + 1) * 256], start=(kc == 0), stop=(kc == 7))
                return last
            p.op("pe", mm, reads=[("w", skv)] + self.hkeys(b), writes=[("ps", bk)])
            k = tb % 2
            kv = sqf[:, k * 512:(k + 1) * 512]
            AKV = self.cfg.get("akv", 7)
            if AKV & 1:
                p.op("act", lambda e, kv=kv, bk=bk: e.activation(out=kv, in_=ps[:, bk, :], func=AF.Copy),
                     reads=[("ps", bk)], writes=["sq", ("kvst", k)])
            if AKV & 2:
                self.out_toks.append(p.dma("sp", f"dkvo{k}", out=d["nk_d"][tb * 128:(tb + 1) * 128, :], in_=kv[:, 0:256],
                                           reads=[("kvst", k), "sq"]))
                self.out_toks.append(p.dma("sp", f"dkvo{k}", out=d["nv_d"][tb * 128:(tb + 1) * 128, :], in_=kv[:, 256:512],
                                           reads=[("kvst", k), "sq"]))
            if AKV & 4:
                bv = self.bankA()
                def mmv(e, tcol=tcol, bv=bv):
                    last = None
                    for kc in range(8):
                        last = e.matmul(ps[:, bv, :], lhsT=h[:, kc, tcol:tcol + 128], rhs=wvd[:, kc, :],
                                        start=(kc == 0), stop=(kc == 7))
                    return last
                p.op("pe", mmv, reads=[("w", svd)] + self.hkeys(b), writes=[("ps", bv)])
                p.op("dve", lambda e, tb=tb, bv=bv: e.tensor_copy(
                    out=Vp[:, tb, :, :].rearrange("p g d -> p (g d)"), in_=ps[:, bv, :]),
                    reads=[("ps", bv)], writes=[("Vp", tb, 0), ("Vp", tb, 1)])
        if ASTAGE < 0.4:
            return
        segs = [(0, 0, 128, "hh", 0), (0, 128, 512, "h", 1033), (1, 0, 128, "h", 1417), (1, 128, 256, "hh", 128)]
        for g in range(4):
            k1, k2, r1, r2 = self.bankA(), self.bankA(), self.bankA(), self.bankA()
            def mm(e, g=g, k1=k1, k2=k2, r1=r1, r2=r2):
                last = None
                for (w_, banks) in ((wkd, (k1, k2)), (wkrd, (r1, r2))):
                    for (bi, c0, c1, src, s0) in segs:
                        n = c1 - c0
                        for kc in range(8):
                            rhs = hh[:, kc, s0:s0 + n] if src == "hh" else h[:, kc, s0:s0 + n]
                            last = e.matmul(ps[:, banks[bi], c0:c1], lhsT=w_[:, kc, g * 128:(g + 1) * 128], rhs=rhs,
                                            start=(kc == 0), stop=(kc == 7))
                return last
            p.op("pe", mm, reads=[("w", skd), ("w", skr)] + hs_keys + hhk, writes=[("ps", b_) for b_ in (k1, k2, r1, r2)])
            for (kb_, rb_, lc0, n) in ((k1, r1, 0, 512), (k2, r2, 512, 256)):
                p.op("dve", lambda e, kb_=kb_, lc0=lc0, n=n: e.tensor_tensor(
                    out=tA[:, lc0:lc0 + n], in0=ps[:, kb_, 0:n], in1=ropec[:, lc0:lc0 + n], op=ALU.mult),
                    reads=[("ps", kb_)] + RK, writes=["sq"])
                p.op("dve", lambda e, rb_=rb_, lc0=lc0, n=n: e.tensor_tensor(
                    out=tB[:, lc0:lc0 + n], in0=ps[:, rb_, 0:n], in1=ropes[:, lc0:lc0 + n], op=ALU.mult),
                    reads=[("ps", rb_)] + RK, writes=["sq"])
                p.op("dve", lambda e, g=g, lc0=lc0, n=n: e.tensor_tensor(
                    out=KTs[:, g, lc0:lc0 + n], in0=tA[:, lc0:lc0 + n], in1=tB[:, lc0:lc0 + n], op=ALU.add),
                    reads=[], writes=["sq", ("KTs", g)])
        if ASTAGE < 0.5:
            return
        for m in range(6):
            bk = self.bankA()
            def mm(e, m=m, bk=bk):
                last = None
                for kc in range(8):
                    if m == 0:
                        lt = hh[:, kc, 0:128]
                    elif m == 5:
                        lt = hh[:, kc, 128:256]
                    else:
                        lt = h[:, kc, 1033 + (m - 1) * 128:1033 + m * 128]
                    last = e.matmul(ps[:, bk, :], lhsT=lt, rhs=wvd[:, kc, :], start=(kc == 0), stop=(kc == 7))
                return last
            p.op("pe", mm, reads=[("w", svd)] + hs_keys + hhk, writes=[("ps", bk)])
            p.op("act" if m % 2 else "dve",
                 (lambda e, m=m, bk=bk: e.activation(out=Vs[:, m, :, :].rearrange("p g d -> p (g d)"), in_=ps[:, bk, :],
                                                     func=AF.Copy)) if m % 2 else
                 (lambda e, m=m, bk=bk: e.tensor_copy(out=Vs[:, m, :, :].rearrange("p g d -> p (g d)"), in_=ps[:, bk, :])),
                 reads=[("ps", bk)], writes=[("Vs", m, 0), ("Vs", m, 1)])
        if ASTAGE < 0.6:
            return
        for dup in range(2):
            for kb in range(4):
                p.dma("pool", "dc_ckd", out=ckd[:, kb, :, dup * 64:(dup + 1) * 64],
                      in_=d["ck_d"][kb * 128:(kb + 1) * 128, :].rearrange("p (g d) -> p g d", g=4),
                      reads=["Ubar"], writes=[("ckd", dup, kb)])
                p.dma("pool", "dc_vc", out=Vc[:, kb, :, dup * 64:(dup + 1) * 64],
                      in_=d["cv_d"][kb * 128:(kb + 1) * 128, :].rearrange("p (g d) -> p g d", g=4),
                      reads=["Ubar"], writes=[("Vc", dup, kb)])
        for nm_ in ("ckd", "Vc"):
            ftok = ("dc_" + ("ckd" if nm_ == "ckd" else "vc"), p.cnt["dc_" + ("ckd" if nm_ == "ckd" else "vc")])
            for dup in range(2):
                for kb in range(4):
                    p.res[(nm_, dup, kb)] = [ftok, []]
        for g in range(4):
            bk = self.bankA()
            def mm(e, g=g, bk=bk):
                last = None
                for kb in range(4):
                    last = e.matmul(ps[:, bk, kb * 128:(kb + 1) * 128], lhsT=ckd[:, kb, g, :], rhs=identb[:, :],
                                    start=True, stop=True)
                return last
            p.op("pe", mm, reads=[("ckd", a_, b_) for a_ in range(2) for b_ in range(4)] + ["identb"], writes=[("ps", bk)])
            p.op("act", lambda e, g=g, bk=bk: e.activation(out=KTc[:, g, :], in_=ps[:, bk, :], func=AF.Copy),
                 reads=[("ps", bk)], writes=[("KTc", g)])

        if ASTAGE < 2:
            return
        saveA, saveiA = self.rotA, self.iA
        self.rotA, self.iA = [0, 1, 2], 0
        ipt = 0
        first_ot = True
        for i in range(NCH):
            ii, g = i % 4, i // 2
            if ii == 0:
                sq_, (wq,) = self.wload([(d["wqkv_d"][:, (i // 4) * 512:(i // 4 + 1) * 512], 8, 512)])
                sr_, (wqr,) = self.wload([(d["wqkr_d"][:, (i // 4) * 512:(i // 4 + 1) * 512], 8, 512)])
            for b in range(4):
                tc0 = self.tokc(b)
                bk = self.bankA()
                def mm(e, ii=ii, tc0=tc0, bk=bk, wq=wq):
                    last = None
                    for kc in range(8):
                        last = e.matmul(ps[:, bk, 0:256], lhsT=wq[:, kc, ii * 128:(ii + 1) * 128], rhs=h[:, kc, tc0:tc0 + 256],
                                        start=(kc == 0), stop=(kc == 7))
                    return last
                p.op("pe", mm, reads=[("w", sq_)] + self.hkeys(b), writes=[("ps", bk)])
                p.op("act", lambda e, b=b, bk=bk: e.activation(out=QT[:, b * 256:(b + 1) * 256], in_=ps[:, bk, 0:256], func=AF.Copy),
                     reads=[("ps", bk)], writes=[("QT", b)])
            q1, q2 = self.bankA(), self.bankA()
            def mm(e, ii=ii, q1=q1, q2=q2, wq=wq, wqr=wqr):
                last = None
                for (w_, bk) in ((wq, q1), (wqr, q2)):
                    for kc in range(8):
                        last = e.matmul(ps[:, bk, :], lhsT=w_[:, kc, ii * 128:(ii + 1) * 128], rhs=h[:, kc, 1033:1545],
                                        start=(kc == 0), stop=(kc == 7))
                return last
            p.op("pe", mm, reads=[("w", sq_), ("w", sr_)] + hs_keys, writes=[("ps", q1), ("ps", q2)])
            p.op("dve", lambda e, q1=q1: e.tensor_tensor(out=tA[:, 0:512], in0=ps[:, q1, :], in1=ropec[:, 128:640], op=ALU.mult),
                 reads=[("ps", q1)] + RK, writes=["sq"])
            p.op("dve", lambda e, q2=q2: e.tensor_tensor(out=tB[:, 0:512], in0=ps[:, q2, :], in1=ropes[:, 128:640], op=ALU.mult),
                 reads=[("ps", q2)] + RK, writes=["sq"])
            p.op("dve", lambda e: e.tensor_tensor(out=QT[:, 1024:1536], in0=tA[:, 0:512], in1=tB[:, 0:512], op=ALU.add),
                 reads=[], writes=["sq", ("QT", 4)])

            def evac(hd, bacc, bD, c0, n, ocol, okeys, ak):
                nonlocal first_ot
                rows = slice(hd * 64, hd * 64 + 64)
                rv_ap = rvt[rows, 0:n]
                d_ap = ps[rows, bD, c0:c0 + n]
                a_ap = ps[rows, bacc, c0:c0 + n]
                es_ap = esink[rows, i:i + 1]
                ot_ap = OT[rows, i, ocol:ocol + n]
                p.op("dve", lambda e: e.tensor_scalar(out=rv_ap, in0=d_ap, scalar1=es_ap, scalar2=None, op0=ALU.add),
                     reads=[("psacc", bD, ak), "esink"], writes=["sq"])
                p.op("dve", lambda e: e.reciprocal(out=rv_ap, in_=rv_ap), reads=[], writes=["sq"])
                p.op("dve", lambda e: e.tensor_tensor(out=ot_ap, in0=a_ap, in1=rv_ap, op=ALU.mult),
                     reads=[("psacc", bacc, ak)], writes=["sq"] + okeys + (TMPK if first_ot else []))
                first_ot = False

            for sq in range(4 if ASTAGE >= 3 else 0):
                for hd in range(2):
                    bacc, bD = (3, 5) if hd == 0 else (4, 6)
                    rows = slice(hd * 64, hd * 64 + 64)
                    for kb in range(2):
                        bk = self.bankA()
                        p.op("pe", lambda e, rows=rows, sq=sq, kb=kb, bk=bk, g=g: e.matmul(
                            ps[:, bk, 0:256], lhsT=KT[rows, g, sq * 256 + kb * 128:sq * 256 + (kb + 1) * 128],
                            rhs=QT[rows, sq * 256:(sq + 1) * 256], start=True, stop=True),
                            reads=[("KT", g, sq), ("QT", sq)], writes=[("ps", bk)])
                        k = ipt % 3
                        ipt += 1
                        p.op("act", lambda e, k=k, bk=bk: e.activation(out=PT[:, k, 0:256], in_=ps[:, bk, 0:256], func=AF.Exp,
                                                                       scale=0.125),
                             reads=[("ps", bk)], writes=[("PT", k)])
                        def pv(e, k=k, sq=sq, kb=kb, hd=hd, bacc=bacc, bD=bD, g=g):
                            e.matmul(ps[:, bacc, 0:256], lhsT=Vp[:, sq * 2 + kb, g, :], rhs=PT[:, k, 0:256],
                                     start=(kb == 0), stop=(kb == 1))
                            return e.matmul(ps[:, bD, 0:256], lhsT=onesb[:, :], rhs=PT[:, k, 0:256],
                                            start=(kb == 0), stop=(kb == 1))
                        p.op("pe", pv, reads=[("PT", k), ("Vp", sq * 2 + kb, 0), ("Vp", sq * 2 + kb, 1), "onesb"],
                             writes=[("psacc", bacc, 0), ("psacc", bD, 0)] + ([("ps", bacc), ("ps", bD)] if kb == 0 else []))
                    evac(hd, bacc, bD, 0, 256, sq * 256, [("gt", i, sq)], 0)
                    tokd = ("dve", p.cnt["dve"])
                    for bnk in (bacc, bD):
                        p.res.setdefault(("ps", bnk), [None, []])[1].append(tokd)
                        p.res.setdefault(("psacc", bnk, 0), [None, []])[1].append(tokd)
            for hd in range(2 if ASTAGE >= 4 else 0):
                rows = slice(hd * 64, hd * 64 + 64)
                bacc, bD = (3, 5) if hd == 0 else (4, 6)
                items = [("c", kb) for kb in range(4)] + [("l", m) for m in range(6)]
                for idx, (kind, m) in enumerate(items):
                    if kind == "c":
                        q0, nq = 0, 512
                        lk = KTc[rows, g, m * 128:(m + 1) * 128]
                        vt = Vc[:, m, g, :]
                        rk, vk_ = [("KTc", g)], [("Vc", 0, m), ("Vc", 1, m)]
                    else:
                        n0, n1 = max(0, m - 2), min(3, m)
                        q0, nq = n0 * 128, (n1 - n0 + 1) * 128
                        lk = KTs[rows, g, m * 128:(m + 1) * 128]
                        vt = Vs[:, m, g, :]
                        rk, vk_ = [("KTs", g)], [("Vs", m, 0), ("Vs", m, 1)]
                    bk = self.bankA()
                    p.op("pe", lambda e, lk=lk, q0=q0, nq=nq, bk=bk, rows=rows: e.matmul(
                        ps[:, bk, 0:nq], lhsT=lk, rhs=QT[rows, 1024 + q0:1024 + q0 + nq], start=True, stop=True),
                        reads=rk + [("QT", 4)], writes=[("ps", bk)])
                    k = ipt % 3
                    ipt += 1
                    p.op("act", lambda e, k=k, bk=bk, nq=nq: e.activation(out=PT[:, k, 0:nq], in_=ps[:, bk, 0:nq], func=AF.Exp,
                                                                          scale=0.125),
                         reads=[("ps", bk)], writes=[("PT", k)])
                    if kind == "l":
                        for n in range(n0, n1 + 1):
                            mk = None
                            if n == m:
                                mk = 2 if m == 0 else 0
                            elif n == m - 2:
                                mk = 3 if m == 5 else 1
                            if mk is not None:
                                off = n * 128 - q0
                                p.op("dve", lambda e, k=k, off=off, mk=mk: e.tensor_tensor(
                                    out=PT[:, k, off:off + 128], in0=PT[:, k, off:off + 128],
                                    in1=masks[:, mk * 128:(mk + 1) * 128], op=ALU.mult),
                                    reads=[("PT", k), "masks"], writes=[("PT", k)])
                    last_i = len(items) - 1
                    def pv(e, k=k, q0=q0, nq=nq, vt=vt, idx=idx, bacc=bacc, bD=bD):
                        e.matmul(ps[:, bacc, q0:q0 + nq], lhsT=vt, rhs=PT[:, k, 0:nq], start=(idx == 0), stop=(idx == last_i))
                        return e.matmul(ps[:, bD, q0:q0 + nq], lhsT=onesb[:, :], rhs=PT[:, k, 0:nq],
                                        start=(idx == 0), stop=(idx == last_i))
                    p.op("pe", pv, reads=[("PT", k), "onesb"] + vk_,
                         writes=[("psacc", bacc, 0), ("psacc", bD, 0)] + ([("ps", bacc), ("ps", bD)] if idx == 0 else []))
                evac(hd, bacc, bD, 0, 512, 1024, [("gt", i, 4), ("gt", i, 5)], 0)
                tokd = ("dve", p.cnt["dve"])
                for bnk in (bacc, bD):
                    p.res.setdefault(("ps", bnk), [None, []])[1].append(tokd)
                    p.res.setdefault(("psacc", bnk, 0), [None, []])[1].append(tokd)
        self.rotA, self.iA = saveA, saveiA
        if ASTAGE >= 5:
            self.out_proj(l, d["wo_d"], OT, self.gkeys)

    def final_norm(self):
        p, ps, x = self.p, self.ps, self.x
        d = self.dram
        for T in (0, 1, 2):
            rstd = self.rms_stats(T)
            col = T * 512
            bs = (2 * T, 2 * T + 1)
            for c in range(NCH):
                p.op("dve", lambda e, c=c, col=col, rstd=rstd: e.scalar_tensor_tensor(
                    out=x[:, c, col:col + 512], in0=x[:, c, col:col + 512], scalar=self.gfin[:, c:c + 1],
                    in1=rstd, op0=ALU.mult, op1=ALU.mult),
                    reads=self.xkeys([c], bs) + ["gfin", ("rstd", T % 2)], writes=self.xkeys([c], bs))
            for tb in range(4):
                tcol = col + tb * 128
                st = self.istage % 2
                self.istage += 1
                for half in range(2):
                    bk = self.bankA()
                    def tr(e, half=half, bk=bk, tcol=tcol):
                        last = None
                        for cc in range(4):
                            c = half * 4 + cc
                            last = e.transpose(ps[:, bk, cc * 128:(cc + 1) * 128], x[:, c, tcol:tcol + 128],
                                               self.ident[:, :])
                        return last
                    p.op("pe", tr, reads=self.xkeys(range(half * 4, half * 4 + 4), bs) + ["ident"],
                         writes=[("ps", bk)])
                    if half == 0:
                        p.op("dve", lambda e, bk=bk, st=st: e.tensor_copy(out=self.stage[:, st, 0:512], in_=ps[:, bk, :]),
                             reads=[("ps", bk)], writes=[("stage", st, 0), ("stage", st)])
                    else:
                        p.op("act", lambda e, bk=bk, st=st: e.activation(out=self.stage[:, st, 512:1024], in_=ps[:, bk, :],
                                                                        func=AF.Copy),
                             reads=[("ps", bk)], writes=[("stage", st, 1), ("stage", st)])
                if T < 2:
                    dst = d["yp_d"][tcol:tcol + 128, :]
                else:
                    dst = d["ys_d"][tcol - 1024:tcol - 1024 + 128, :]
                tok = p.dma("sp", f"dst{st}", out=dst, in_=self.stage[:, st, :],
                            reads=[("stage", st, 0), ("stage", st, 1), ("stage", st)])
                self.out_toks.append(tok)


def _rope_partner():
    idx = np.arange(64)
    d = idx % 32
    return np.where(d < 16, idx + 16, idx - 16)


def _host_consts(core):
    qd = core % 4
    s0 = qd * 512
    c = {}
    c["ident"] = np.eye(128, dtype=np.float32)
    r = np.arange(128)
    dd = r % 64
    fi = (dd % 16).astype(np.float32)
    inv = (np.float32(10000.0) ** (-fi / np.float32(16.0))).astype(np.float32)
    t_abs = s0 - 128 + np.arange(768)
    pos_row = (t_abs // 64).astype(np.float32)
    pos_col = (t_abs % 64).astype(np.float32)
    pos = np.where((dd < 32)[:, None], pos_row[None, :], pos_col[None, :]).astype(np.float32)
    ang = (pos * inv[:, None]).astype(np.float32)
    sign = np.where((dd % 32) < 16, -1.0, 1.0).astype(np.float32)
    c["rope_cos"] = np.cos(ang).astype(np.float32)
    c["rope_sin"] = (np.sin(ang) * sign[:, None]).astype(np.float32)
    jj = np.arange(128)[:, None]
    ii = np.arange(128)[None, :]
    prev = (jj >= ii).astype(np.float32)
    nxt = (jj <= ii).astype(np.float32)
    vl = 1.0 if qd != 0 else 0.0
    vr = 1.0 if qd != 3 else 0.0
    c["masks"] = np.concatenate([prev, nxt, prev * vl, nxt * vr], axis=1).astype(np.float32)
    sel = np.zeros((128, 8, 8, 2), np.float32)
    if qd != 3:
        sel[:, core + 1, :, 0] = 1.0
    if qd != 0:
        sel[:, core - 1, :, 1] = 1.0
    c["selmask"] = sel.reshape(128, 128)
    k = np.arange(128)
    angc = 2.0 * np.pi * ((k[:, None] * k[None, :]) % 128) / 128.0
    c["dft_c"] = (np.concatenate([np.cos(angc), np.sin(angc)], axis=1) / np.sqrt(128.0)).astype(np.float32)
    t = np.arange(256)
    angp = 2.0 * np.pi * ((t[:, None] * t[None, :]) % 256) / 256.0
    c["dft_tp"] = (np.concatenate([np.cos(angp), -np.sin(angp)], axis=1) / 16.0).astype(np.float32)
    t = np.arange(2048)
    tp = s0 + np.arange(512)
    angs = 2.0 * np.pi * ((t[:, None] * tp[None, :]) % 2048) / 2048.0
    c["dft_ts"] = (np.concatenate([np.cos(angs), -np.sin(angs)], axis=1) / np.sqrt(2048.0)).astype(np.float32)
    return c


def _prep_inputs(I):
    f = lambda a: np.ascontiguousarray(np.asarray(a, dtype=np.float32))
    sh = {}
    sh["ada_w"] = f(I["ada_w"])
    sh["ada_bT"] = f(np.asarray(I["ada_b"]).reshape(4, 48, 128).transpose(2, 0, 1).reshape(128, 192))
    sh["gmixT"] = f(np.asarray(I["norm_mix_g"]).reshape(4, 8, 128).transpose(2, 0, 1).reshape(128, 32))
    sh["gffnT"] = f(np.asarray(I["norm_ffn_g"]).reshape(4, 8, 128).transpose(2, 0, 1).reshape(128, 32))
    sh["gfinT"] = f(np.asarray(I["final_g"]).reshape(8, 128).T)
    wqkv = np.asarray(I["attn_wqkv"])[0]
    sh["wqkv"] = f(wqkv)
    perm = np.concatenate([h * 64 + _rope_partner() for h in range(20)])
    sh["wqk_rot"] = f(wqkv[:, perm])
    sh["wo"] = f(np.asarray(I["attn_wo"])[0])
    wrot = wqkv[:, perm]
    sh["wkd"] = f(np.concatenate([np.concatenate([wqkv[:, 1024 + g * 64:1024 + (g + 1) * 64]] * 2, axis=1) for g in range(4)], axis=1))
    sh["wvd"] = f(np.concatenate([np.concatenate([wqkv[:, 1280 + g * 64:1280 + (g + 1) * 64]] * 2, axis=1) for g in range(4)], axis=1))
    sh["wkrd"] = f(np.concatenate([np.concatenate([wrot[:, 1024 + g * 64:1024 + (g + 1) * 64]] * 2, axis=1) for g in range(4)], axis=1))
    sink = np.asarray(I["attn_sink"])[0]
    pidx = np.arange(128) // 64
    sh["sinkT"] = f(np.stack([sink[2 * i + pidx] for i in range(8)], axis=1))
    sh["sgu_w_in"] = f(np.asarray(I["sgu_w_in"])[0])
    sh["sgu_lng_b"] = f(np.broadcast_to(np.asarray(I["sgu_ln_g"])[0][None, :], (128, 1024)))
    sh["sgu_w_s"] = f(np.asarray(I["sgu_w_s"])[0])
    sh["sgu_bs_b"] = f(np.broadcast_to(np.asarray(I["sgu_b_s"])[0].reshape(1, 1024), (128, 1024)))
    sh["sgu_w_out"] = f(np.asarray(I["sgu_w_out"])[0])
    sh["sc_w_in"] = f(np.asarray(I["sc_w_in"])[0])
    sh["sc_convT"] = f(np.asarray(I["sc_conv"])[0].reshape(3, 8, 128).transpose(2, 1, 0).reshape(128, 24))
    sh["sc_w_out"] = f(np.asarray(I["sc_w_out"])[0])
    sh["fn_w_out"] = f(np.asarray(I["fn_w_out"])[0])
    sh["ffn_w_up"] = f(I["ffn_w_up"])
    sh["ffn_convT"] = f(np.asarray(I["ffn_conv"]).reshape(4, 3, 44, 128).transpose(3, 0, 2, 1).reshape(128, 528))
    sh["ffn_w_down"] = f(I["ffn_w_down"])
    xp = np.asarray(I["x_prompt"], dtype=np.float32)
    xs = np.asarray(I["x_sample"], dtype=np.float32)
    ck = np.asarray(I["cache_k"], dtype=np.float32)
    cv = np.asarray(I["cache_v"], dtype=np.float32)
    cc = np.asarray(I["c"], dtype=np.float32)
    cctx = np.asarray(I["c_ctx"], dtype=np.float32)
    maps = []
    for core in range(8):
        b, qd = core // 4, core % 4
        s0 = qd * 512
        m = dict(sh)
        m.update(_host_consts(core))
        m["xp"] = f(xp[4 * core:4 * core + 4].reshape(1024, 1024))
        m["xs"] = f(xs[b, s0:s0 + 512])
        xsh = np.zeros((256, 1024), np.float32)
        if qd != 0:
            xsh[0:128] = xs[b, s0 - 128:s0]
        if qd != 3:
            xsh[128:256] = xs[b, s0 + 512:s0 + 640]
        m["xsh"] = xsh
        m["ck"] = f(ck[b, 0].reshape(512, 256))
        m["cv"] = f(cv[b, 0].reshape(512, 256))
        m["cT"] = f(np.stack([cctx, cc[b]], axis=-1).reshape(8, 128, 2).transpose(1, 0, 2).reshape(128, 16))
        maps.append(m)
    return maps


_CFG = {}
_NC_CACHE = {}


def _get_nc(cfg):
    key = repr(sorted(cfg.items()))
    if key not in _NC_CACHE:
        b = Builder(cfg)
        _NC_CACHE[key] = (b.build(), b)
    return _NC_CACHE[key]


def kernel(**inputs):
    cfg = dict(_CFG)
    nc, b = _get_nc(cfg)
    maps = _prep_inputs(inputs)
    used = set(b.din.keys())
    maps = [{k: v for k, v in m.items() if k in used} for m in maps]
    res = run_bass_kernel_spmd(nc, maps, core_ids=list(range(8)))
    y_prompt = np.zeros((32, 256, 1024), np.float32)
    y_sample = np.zeros((2, 2048, 1024), np.float32)
    nk = np.zeros((32, 1, 256, 4, 64), np.float32)
    nv = np.zeros((32, 1, 256, 4, 64), np.float32)
    for core in range(8):
        r = res.results[core]
        bb, qd = core // 4, core % 4
        y_prompt[4 * core:4 * core + 4] = np.asarray(r["yp"]).reshape(4, 256, 1024)
        y_sample[bb, qd * 512:(qd + 1) * 512] = np.asarray(r["ys"])
        nk[4 * core:4 * core + 4, 0] = np.asarray(r["nk"]).reshape(4, 256, 4, 64)
        nv[4 * core:4 * core + 4, 0] = np.asarray(r["nv"]).reshape(4, 256, 4, 64)
    return (y_prompt, y_sample, nk, nv)
```
